# Optimizing a Trainium2 kernel written in Bass

```python
import jax, jax.numpy as jnp
from jax import lax
import numpy as np

D_MODEL = 2048
BATCH = 4
SEQ = 4096
DEPTH = 1

POOL_WIDTH = D_MODEL // 2
POOL_WINDOWS = (2, 4, 8, 16)
N_POOL_GROUPS = len(POOL_WINDOWS)
POOL_GROUP = POOL_WIDTH // N_POOL_GROUPS
CONV_WIDTH = D_MODEL // 2
CONV_KERNEL = 31
D_FF = ((8 * D_MODEL // 3 + 255) // 256) * 256
N_BRANCHES = 2
IN_WIDTH = POOL_WIDTH + 2 * CONV_WIDTH + N_BRANCHES * D_MODEL
FFN_RESIDUAL_WEIGHT = 0.5
EPS = 1e-6

kernel_name = "gated_pool_conformer_macaron_block"


def rmsnorm(x, g):
    xf = x.astype(jnp.float32)
    y = xf * lax.rsqrt(jnp.mean(xf * xf, axis=-1, keepdims=True) + EPS)
    return (y * g.astype(jnp.float32)).astype(x.dtype)


def layernorm(x, g, b):
    xf = x.astype(jnp.float32)
    mu = jnp.mean(xf, axis=-1, keepdims=True)
    var = jnp.mean(jnp.square(xf - mu), axis=-1, keepdims=True)
    y = (xf - mu) * lax.rsqrt(var + EPS)
    return (y * g.astype(jnp.float32) + b.astype(jnp.float32)).astype(x.dtype)


def swiglu_ffn(h, w_in, w_out):
    gate, up = jnp.split(h @ w_in, 2, axis=-1)
    return (jax.nn.silu(gate) * up) @ w_out


def causal_multiscale_pool(u):
    S = u.shape[1]
    uf = u.astype(jnp.float32)
    cs = jnp.pad(jnp.cumsum(uf, axis=1), ((0, 0), (1, 0), (0, 0)))
    t = jnp.arange(S)
    outs = []
    for g, w in enumerate(POOL_WINDOWS):
        sl = slice(g * POOL_GROUP, (g + 1) * POOL_GROUP)
        lo = jnp.maximum(t + 1 - w, 0)
        window_sum = cs[:, 1:, sl] - jnp.take(cs[:, :, sl], lo, axis=1)
        count = jnp.minimum(t + 1, w).astype(jnp.float32)[None, :, None]
        outs.append(window_sum / count - uf[..., sl])
    return jnp.stack(outs, axis=2).astype(u.dtype)


def causal_depthwise_conv(u, k, b):
    C = u.shape[-1]
    y = lax.conv_general_dilated(
        u, k.reshape(CONV_KERNEL, 1, C).astype(u.dtype),
        window_strides=(1,), padding=[(CONV_KERNEL - 1, 0)],
        dimension_numbers=("NWC", "WIO", "NWC"), feature_group_count=C)
    return y + b


def mixer_block(h, w_in, b_in, pool_w_grp, pool_scale, pool_w_proj,
                conv_dw, conv_b, conv_ln_g, conv_ln_b, conv_w_proj, w_out):
    B, S, _ = h.shape
    z = h @ w_in + b_in
    u_pool = z[..., :POOL_WIDTH]
    u_conv = z[..., POOL_WIDTH:POOL_WIDTH + 2 * CONV_WIDTH]
    g_logits = z[..., POOL_WIDTH + 2 * CONV_WIDTH:]
    pooled = causal_multiscale_pool(u_pool)
    mixed = jnp.einsum("bsgc,gcd->bsgd", pooled, pool_w_grp).reshape(B, S, POOL_WIDTH)
    a = (mixed * pool_scale) @ pool_w_proj
    glu = u_conv[..., :CONV_WIDTH] * jax.nn.sigmoid(u_conv[..., CONV_WIDTH:])
    c = causal_depthwise_conv(glu, conv_dw, conv_b)
    c = jax.nn.silu(layernorm(c, conv_ln_g, conv_ln_b))
    bb = c @ conv_w_proj
    g_a, g_b = jnp.split(g_logits, N_BRANCHES, axis=-1)
    merged = jax.nn.sigmoid(g_a) * a + jax.nn.sigmoid(g_b) * bb
    return merged @ w_out


def setup_inputs(seed: int = 0) -> dict:
    key = jax.random.key(seed)
    ks = jax.random.split(key, 24)
    f32 = jnp.float32

    def nrm(k, shape, fan_in, scale=1.0):
        return (jax.random.normal(k, shape, f32) * (scale * fan_in ** -0.5)).astype(f32)

    def gain(k, shape):
        return (1.0 + 0.02 * jax.random.normal(k, shape, f32)).astype(f32)

    L = DEPTH
    return {
        "x": jax.random.normal(ks[0], (BATCH, SEQ, D_MODEL), f32),
        "ffn1_norm": gain(ks[1], (L, D_MODEL)),
        "ffn1_w_in": nrm(ks[2], (L, D_MODEL, 2 * D_FF), D_MODEL),
        "ffn1_w_out": nrm(ks[3], (L, D_FF, D_MODEL), D_FF),
        "mix_norm": gain(ks[4], (L, D_MODEL)),
        "w_in": nrm(ks[5], (L, D_MODEL, IN_WIDTH), D_MODEL),
        "b_in": (0.02 * jax.random.normal(ks[6], (L, IN_WIDTH), f32)).astype(f32),
        "pool_w_grp": nrm(ks[7], (L, N_POOL_GROUPS, POOL_GROUP, POOL_GROUP), POOL_GROUP),
        "pool_scale": gain(ks[8], (L, POOL_WIDTH)),
        "pool_w_proj": nrm(ks[9], (L, POOL_WIDTH, D_MODEL), POOL_WIDTH),
        "conv_dw": nrm(ks[10], (L, CONV_KERNEL, CONV_WIDTH), CONV_KERNEL),
        "conv_b": (0.02 * jax.random.normal(ks[11], (L, CONV_WIDTH), f32)).astype(f32),
        "conv_ln_g": gain(ks[12], (L, CONV_WIDTH)),
        "conv_ln_b": (0.02 * jax.random.normal(ks[13], (L, CONV_WIDTH), f32)).astype(f32),
        "conv_w_proj": nrm(ks[14], (L, CONV_WIDTH, D_MODEL), CONV_WIDTH),
        "w_out": nrm(ks[15], (L, D_MODEL, D_MODEL), D_MODEL),
        "ffn2_norm": gain(ks[16], (L, D_MODEL)),
        "ffn2_w_in": nrm(ks[17], (L, D_MODEL, 2 * D_FF), D_MODEL),
        "ffn2_w_out": nrm(ks[18], (L, D_FF, D_MODEL), D_FF),
        "final_norm": gain(ks[19], (D_MODEL,)),
    }


def reference(x, ffn1_norm, ffn1_w_in, ffn1_w_out, mix_norm, w_in, b_in,
              pool_w_grp, pool_scale, pool_w_proj, conv_dw, conv_b, conv_ln_g, conv_ln_b,
              conv_w_proj, w_out, ffn2_norm, ffn2_w_in, ffn2_w_out, final_norm):
    for l in range(DEPTH):
        x = x + FFN_RESIDUAL_WEIGHT * swiglu_ffn(rmsnorm(x, ffn1_norm[l]), ffn1_w_in[l], ffn1_w_out[l])
        x = x + mixer_block(rmsnorm(x, mix_norm[l]), w_in[l], b_in[l],
                            pool_w_grp[l], pool_scale[l], pool_w_proj[l],
                            conv_dw[l], conv_b[l], conv_ln_g[l], conv_ln_b[l],
                            conv_w_proj[l], w_out[l])
        x = x + FFN_RESIDUAL_WEIGHT * swiglu_ffn(rmsnorm(x, ffn2_norm[l]), ffn2_w_in[l], ffn2_w_out[l])
    return rmsnorm(x, final_norm)
```

```python
from contextlib import ExitStack
import numpy as np
import concourse.bass as bass
import concourse.mybir as mybir
from concourse.bass_utils import run_bass_kernel_spmd

F32 = mybir.dt.float32
BF16 = mybir.dt.bfloat16
AF = mybir.ActivationFunctionType
ALU = mybir.AluOpType

D = 2048
DFF = 5632
NKD = D // 128
NKF = DFF // 128
SEQ = 4096
BATCH = 4
TOK_CORE = 2048
T = 512
HALO = 32
WCOL = T + HALO
POOLW = 1024
CONVW = 1024
KCONV = 31
INW = POOLW + 2 * CONVW + 2 * D
EPS = 1e-6
RING = 5
NBANK = 8

PE, ACT, DVE, POOL, SP = "tensor", "scalar", "vector", "gpsimd", "sync"
ENGS = (PE, ACT, DVE, POOL, SP)

C_N1 = 0
C_N2 = C_N1 + NKD
C_N3 = C_N2 + NKD
C_NF = C_N3 + NKD
C_BIN = C_NF + NKD
C_PSC = C_BIN + INW // 128
C_CVB = C_PSC + 8
C_LNG = C_CVB + 8
C_LNB = C_LNG + 8
C_CVK = C_LNB + 8
C_MASK = C_CVK + 8 * KCONV
C_EPS = C_MASK + 1
C_INVC = C_EPS + 1
NCST = C_INVC + 4 * HALO


class Buf:
    __slots__ = ("name", "w", "r")

    def __init__(self, name):
        self.name = name
        self.w = None
        self.r = []


class Prog:
    def __init__(self, nc, stack, dry=False):
        self.nc = nc
        self.stack = stack
        self.dry = dry
        self.q = {e: [] for e in ENGS}
        self.sems = {}
        self.cnt = {}
        self.seen = {e: {} for e in ENGS}
        for e in ENGS:
            self.newsem(e)
        self.n_ops = 0

    def newsem(self, key):
        if not self.dry:
            self.sems[key] = self.stack.enter_context(self.nc.semaphore("s_" + str(key)))
        self.cnt[key] = 0
        return key

    def _deps(self, eng, reads, writes):
        need = {}

        def add(tok, raw):
            if tok is None:
                return
            key, val, teng = tok
            if teng == eng and key == eng and eng == PE:
                return
            if need.get(key, 0) < val:
                need[key] = val

        for b in reads:
            add(b.w, True)
        for b in writes:
            add(b.w, False)
            for t in b.r:
                add(t, False)
        out = []
        for key, val in need.items():
            if self.seen[eng].get(key, 0) < val:
                self.seen[eng][key] = val
                out.append((key, val))
        return out

    def op(self, eng, fn, reads=(), writes=(), semkey=None, incval=1):
        if self.dry:
            return None
        waits = self._deps(eng, reads, writes)
        key = semkey if semkey is not None else eng
        self.cnt[key] += incval
        tok = (key, self.cnt[key], eng)
        sems = self.sems
        sem = sems[key]

        def emit(e, waits=waits, fn=fn, sem=sem, incval=incval):
            for k, v in waits:
                e.wait_ge(sems[k], v)
            fn(e).then_inc(sem, incval)

        self.q[eng].append(emit)
        self.n_ops += 1
        for b in writes:
            b.w = tok
            b.r = []
        for b in reads:
            b.r.append(tok)
        return tok

    def wait_for(self, eng, bufs):
        if self.dry:
            return
        waits = self._deps(eng, bufs, ())
        sems = self.sems

        def emit(e, waits=waits):
            for k, v in waits:
                e.wait_ge(sems[k], v)

        self.q[eng].append(emit)

    def emit_all(self):
        with self.nc.Block() as block:
            for ename in ENGS:
                lst = self.q[ename]
                if not lst:
                    continue

                def body(e, lst=lst):
                    for f in lst:
                        f(e)

                getattr(block, ename)(body)


class WStream:
    def __init__(self, P, ring):
        self.P = P
        self.ring = ring
        self.n = 0
        self.bufs = [Buf(f"ring{i}") for i in range(RING)]
        self.semk = [P.newsem(f"ring{i}") for i in range(RING)]

    def get(self, wt, k0, nk, c0):
        n = self.n
        self.n += 1
        s = n % RING
        src = wt[k0 * 128:(k0 + nk) * 128, c0:c0 + 256].rearrange("(kc p) c -> p kc c", p=128)
        dst = self.ring[s]
        self.P.op(POOL, lambda e, dst=dst, src=src, nk=nk: e.dma_start(out=dst[:, 0:nk, :], in_=src),
                  writes=[self.bufs[s]], semkey=self.semk[s], incval=16)
        return s


def emit_program(nc, st, dr, n_tiles):
    dry = False
    P = Prog(nc, st)

    def sb(name, shape, dt):
        if dry:
            return None
        return st.enter_context(nc.sbuf_tensor(name, shape, dt))

    X = sb("X", [128, NKD, WCOL], F32)
    H = sb("H", [128, NKD, WCOL], BF16)
    SCRW = 16768
    SCR = sb("SCR", [128, SCRW], F32)
    CST = sb("CST", [128, NCST], F32)
    ONES = sb("ONES", [128, 128], F32)
    NTMP = 6
    TMP = [sb(f"TMP{i}", [128, WCOL], F32) for i in range(NTMP)]
    PT = [sb(f"PT{i}", [128, 2, T + 16], F32) for i in range(2)]
    RS = sb("RS", [128, WCOL], F32)
    ACC = {"m": sb("ACCm", [128, T], F32), "h": sb("ACCh", [128, HALO], F32)}
    ACC2 = sb("ACC2", [128, T], F32)
    MEAN = sb("MEAN", [128, T], F32)
    GH = sb("GH", [128, 8, HALO], F32)
    SGBT = sb("SGBT", [128, NKD * T], BF16)
    RS2 = sb("RS2", [128, T], F32)
    UH = sb("UH", [128, 8, 16], F32)
    ring = [sb(f"ring{i}", [128, 16, 256], BF16) for i in range(RING)]
    if not dry:
        banks = [st.enter_context(nc.psum_tensor(f"bank{i}", [128, 512], F32)) for i in range(8)]
        ACTB = SCR[:, 0:NKF * WCOL // 2].bitcast(BF16).rearrange("p (c w) -> p c w", w=WCOL)
        o = 0
        G = SCR[:, o:o + 8 * WCOL].rearrange("p (c w) -> p c w", w=WCOL)
        OUTB = SCR[:, 3808:3808 + NKD * T].rearrange("p (c w) -> p c w", w=T)
        o += 8 * WCOL
        C = SCR[:, o:o + 8 * T].rearrange("p (c w) -> p c w", w=T)
        o += 8 * T
        U = SCR[:, o:o + 8 * (T + 16)].rearrange("p (c w) -> p c w", w=T + 16)
        MERGED = SCR[:, o:o + 8 * T].bitcast(BF16).rearrange("p (c w) -> p c w", w=T)
        o += 8 * (T + 16)
        PLD = SCR[:, o:o + 4 * T].bitcast(BF16).rearrange("p (c w) -> p c w", w=T)
        CN = PLD
        o += 4 * T
        MX = SCR[:, o:o + 4 * T].bitcast(BF16).rearrange("p (c w) -> p c w", w=T)
        o += 4 * T
        assert o <= SCRW and NKF * WCOL // 2 <= SCRW
        SGB = SGBT[:].rearrange("p (c w) -> p c w", w=T)
        XF_lo = SGBT[:].bitcast(F32).rearrange("p (c w) -> p c w", w=T)
        XF_hi = SCR[:, 12000:12000 + 8 * T].rearrange("p (c w) -> p c w", w=T)
    W = WStream(P, ring)

    b_X = {"m": [Buf(f"Xm{j}") for j in range(NKD)], "h": [Buf(f"Xh{j}") for j in range(NKD)]}
    b_H = {"m": [Buf(f"Hm{j}") for j in range(NKD)], "h": [Buf(f"Hh{j}") for j in range(NKD)]}
    b_G = [Buf(f"G{j}") for j in range(8)]
    b_C = [Buf(f"C{j}") for j in range(8)]
    b_U = [Buf(f"U{j}") for j in range(8)]
    b_PC = [Buf(f"PC{j}") for j in range(8)]
    b_MX = [Buf(f"MX{j}") for j in range(8)]
    b_ACT = b_G + b_C + b_U
    bA_lo = b_G[0:7]
    bA_hi = [b_G[7]] + b_C + b_U[0:7]
    b_CST = Buf("CST")
    b_ONES = Buf("ONES")
    b_TMP = [Buf(f"TMP{i}") for i in range(NTMP)]
    b_PT = [Buf(f"PT{i}") for i in range(2)]
    b_RS = {"m": Buf("RSm"), "h": Buf("RSh")}
    b_RS2 = Buf("RS2")
    b_MEAN = Buf("MEAN")
    b_ACC = {"m": Buf("ACCm"), "h": Buf("ACCh")}
    b_ACC2 = Buf("ACC2")
    b_M = [Buf(f"M{j}") for j in range(NKD)]
    b_GH = Buf("GH")
    b_SGB = [Buf(f"SGB{j}") for j in range(NKD)]
    b_UH = Buf("UH")
    b_bank = [Buf(f"bank{i}") for i in range(NBANK)]
    b_hs = [Buf(f"hs{i}") for i in range(16)]
    s_xq = [P.newsem(f"xload{q}") for q in range(4)]
    s_c = P.newsem("cload")
    s_xh = P.newsem("xhload")
    s_o = P.newsem("ostore")
    b_OUT = Buf("OUTDRAM")

    state = {"bank": 0, "hs": 0, "tmp": 0, "pt": 0, "stream": False}

    def nbank():
        i = state["bank"]
        state["bank"] = (i + 1) % NBANK
        return i

    def nhs():
        i = state["hs"]
        state["hs"] = (i + 1) % 16
        return i

    def ntmp():
        i = state["tmp"]
        state["tmp"] = (i + 1) % NTMP
        return i

    def npt():
        i = state["pt"]
        state["pt"] = (i + 1) % 2
        return i

    def cols(g):
        return (0, T) if g == "m" else (T, WCOL)

    def ncols(g):
        return T if g == "m" else HALO

    def cst(c):
        return CST[:, c:c + 1]

    class PS:
        def __init__(self, g, full=False):
            self.g = g
            self.n = ncols(g)
            if True:
                self.i = nbank()
                self.buf = b_bank[self.i]
                self.base = banks[self.i]
                self.o = 0
            else:
                self.i = nhs()
                self.buf = b_hs[self.i]
                self.base = banks[7]
                self.o = self.i * HALO

        def ap(self, lo=0, hi=None):
            hi = self.n if hi is None else hi
            return self.base[:, self.o + lo:self.o + hi]

    P.op(SP, lambda e: e.dma_start(out=CST[:], in_=dr["cst"]), writes=[b_CST], semkey=s_c, incval=16)
    P.op(DVE, lambda e: e.memset(ONES[:], 1.0), writes=[b_ONES])

    def proj(wt, nkc, c0, rhs, rhs_bufs, groups):
        full = nkc > 16
        dst = {g: [PS(g, full), PS(g, full)] for g in groups}
        stream = state["stream"] and rhs is rhs_H
        if stream:
            state["stream"] = False
        for kb in range(0, nkc, 16):
            nk = min(16, nkc - kb)
            s = W.get(wt, kb, nk, c0)
            allr = [W.bufs[s]] + [b for g in groups for b in rhs_bufs[g]]
            if stream:
                d0 = dst["m"][0]
                for kc in range(nk):
                    P.op(PE, lambda e, s=s, kc=kc, d0=d0: e.matmul(
                        d0.ap(), ring[s][:, kc, 0:128], rhs(kc, "m"), start=(kc == 0), stop=(kc == nkc - 1)),
                        reads=[W.bufs[s], rhs_bufs["m"][kc]], writes=[d0.buf])

            def mm(e, s=s, kb=kb, nk=nk, stream=stream):
                ins = None
                for m in range(2):
                    for g in groups:
                        if stream and m == 0 and g == "m":
                            continue
                        d = dst[g][m].ap()
                        for kc in range(nk):
                            k = kb + kc
                            ins = e.matmul(d, ring[s][:, kc, m * 128:(m + 1) * 128], rhs(k, g),
                                           start=(k == 0), stop=(k == nkc - 1))
                return ins

            P.op(PE, mm, reads=allr, writes=[dst[g][m].buf for g in groups for m in range(2)])
        return dst

    def stats(src, src_bufs, nch, g, direct=False):
        d = PS(g)
        n = ncols(g)
        if direct:
            for j in range(nch):
                t = ntmp()
                P.op(ACT, lambda e, t=t, j=j: e.activation(out=TMP[t][:, 0:n], in_=src(j, g), func=AF.Square),
                     reads=[src_bufs(j, g)], writes=[b_TMP[t]])
                P.op(PE, lambda e, t=t, j=j: e.matmul(d.ap(), ONES[:], TMP[t][:, 0:n],
                                                       start=(j == 0), stop=(j == nch - 1)),
                     reads=[b_TMP[t], b_ONES], writes=[d.buf])
            return d
        acc = ACC[g]
        t0 = None
        for j in range(nch):
            t = ntmp()
            P.op(ACT, lambda e, t=t, j=j: e.activation(out=TMP[t][:, 0:n], in_=src(j, g), func=AF.Square),
                 reads=[src_bufs(j, g)], writes=[b_TMP[t]])
            if j == 0:
                t0 = t
            elif j == 1:
                P.op(DVE, lambda e, t=t, t0=t0: e.tensor_tensor(out=acc[:, 0:n], in0=TMP[t0][:, 0:n], in1=TMP[t][:, 0:n],
                                                                op=ALU.add),
                     reads=[b_TMP[t0], b_TMP[t]], writes=[b_ACC[g]])
            else:
                P.op(DVE, lambda e, t=t: e.tensor_tensor(out=acc[:, 0:n], in0=acc[:, 0:n], in1=TMP[t][:, 0:n], op=ALU.add),
                     reads=[b_ACC[g], b_TMP[t]], writes=[b_ACC[g]])
        P.op(PE, lambda e: e.matmul(d.ap(), ONES[:], acc[:, 0:n], start=True, stop=True),
             reads=[b_ACC[g], b_ONES], writes=[d.buf])
        return d

    def rstd_from(d, n, g, dst_ap, dst_buf):
        P.op(ACT, lambda e: e.activation(out=dst_ap, in_=d.ap(), func=AF.Sqrt, bias=cst(C_EPS), scale=1.0 / n),
             reads=[d.buf, b_CST], writes=[dst_buf])
        P.op(DVE, lambda e: e.reciprocal(out=dst_ap, in_=dst_ap), reads=[dst_buf], writes=[dst_buf])

    def XF(j):
        return XF_lo[:, j, :] if j < 8 else XF_hi[:, j - 8, :]

    def xf_bufs(j):
        if j < 8:
            return [b_SGB[2 * j], b_SGB[2 * j + 1]]
        return [b_U[6], b_U[7]] + b_PC + b_MX + b_M

    class XStats:
        def __init__(self, groups, from_xf=False, alt=False, hp_cg=None):
            self.groups = groups
            self.from_xf = from_xf
            self.hp_cg = hp_cg
            self.acc = {"m": ACC2, "h": None} if alt else ACC
            self.accb = {"m": b_ACC2, "h": None} if alt else b_ACC
            self.pend = {g: [] for g in groups}
            self.n = {g: 0 for g in groups}
            self.first = {g: None for g in groups}

        def flush(self):
            for g in self.groups:
                n = ncols(g)
                acc = self.acc[g]
                b_acc = self.accb[g]
                for t in self.pend[g]:
                    k = self.n[g]
                    self.n[g] += 1
                    if k == 0:
                        self.first[g] = t
                    elif k == 1:
                        t0 = self.first[g]
                        P.op(DVE, lambda e, t=t, t0=t0, acc=acc, n=n: e.tensor_tensor(
                            out=acc[:, 0:n], in0=TMP[t0][:, 0:n], in1=TMP[t][:, 0:n], op=ALU.add),
                            reads=[b_TMP[t0], b_TMP[t]], writes=[b_acc])
                    else:
                        P.op(DVE, lambda e, t=t, acc=acc, n=n: e.tensor_tensor(
                            out=acc[:, 0:n], in0=acc[:, 0:n], in1=TMP[t][:, 0:n], op=ALU.add),
                            reads=[b_acc, b_TMP[t]], writes=[b_acc])
                self.pend[g] = []

        def feed(self, js):
            self.flush()
            for g in self.groups:
                lo, hi = cols(g)
                n = hi - lo
                for j in js:
                    t = ntmp()
                    if self.from_xf:
                        P.op(ACT, lambda e, t=t, j=j: e.activation(out=TMP[t][:, 0:T], in_=XF(j), func=AF.Square),
                             reads=xf_bufs(j), writes=[b_TMP[t]])
                    else:
                        P.op(ACT, lambda e, t=t, j=j, lo=lo, hi=hi, n=n: e.activation(
                            out=TMP[t][:, 0:n], in_=X[:, j, lo:hi], func=AF.Square),
                            reads=[b_X[g][j]], writes=[b_TMP[t]])
                        if self.hp_cg is not None:
                            cgj = self.hp_cg + j
                            P.op(ACT, lambda e, j=j, lo=lo, hi=hi, cgj=cgj: e.activation(
                                out=H[:, j, lo:hi], in_=X[:, j, lo:hi], func=AF.Copy, scale=cst(cgj)),
                                reads=[b_X[g][j], b_CST], writes=[b_H[g][j]])
                    self.pend[g].append(t)

        def finish(self):
            self.flush()
            out = {}
            for g in self.groups:
                n = ncols(g)
                d = PS(g)
                acc = self.acc[g]
                P.op(PE, lambda e, d=d, acc=acc, n=n: e.matmul(d.ap(), ONES[:], acc[:, 0:n], start=True, stop=True),
                     reads=[self.accb[g], b_ONES], writes=[d.buf])
                out[g] = d
            return out

    def rmsnorm_to_H(cg, groups, dst):
        for g in groups:
            lo, hi = cols(g)
            rstd_from(dst[g], D, g, RS[:, lo:hi], b_RS[g])
            for j in range(NKD):
                P.op(DVE, lambda e, j=j, lo=lo, hi=hi: e.scalar_tensor_tensor(
                    out=H[:, j, lo:hi], in0=X[:, j, lo:hi], scalar=cst(cg + j), in1=RS[:, lo:hi],
                    op0=ALU.mult, op1=ALU.mult),
                    reads=[b_X[g][j], b_RS[g], b_CST], writes=[b_H[g][j]])
        state["stream"] = False

    def rhs_H(k, g):
        lo, hi = cols(g)
        return H[:, k, lo:hi]

    H_bufs = {"m": b_H["m"], "h": b_H["h"]}

    def ffn(cg, w_in, w_out, groups, xs_in, next_groups, final=False, after_norm=None, next_load=None):
        def norm_stats():
            dst = xs_in.finish()
            for g in groups:
                lo, hi = cols(g)
                rstd_from(dst[g], D, g, RS[:, lo:hi], b_RS[g])

        early = len(groups) > 1
        if early:
            norm_stats()
        for b in range(NKF // 2):
            dg = proj(w_in, NKD, b * 256, rhs_H, H_bufs, groups)
            du = proj(w_in, NKD, DFF + b * 256, rhs_H, H_bufs, groups)
            if b == 0 and not early:
                norm_stats()
            work = [(m, g) for m in range(2) for g in groups]
            ts = {}
            for (m, g) in work:
                lo, hi = cols(g)
                n = hi - lo
                t = ntmp()
                ts[(m, g)] = t
                P.op(DVE, lambda e, t=t, pg=dg[g][m], n=n, lo=lo, hi=hi: e.tensor_tensor(
                    out=TMP[t][:, 0:n], in0=pg.ap(), in1=RS[:, lo:hi], op=ALU.mult),
                    reads=[dg[g][m].buf, b_RS[g]], writes=[b_TMP[t]])
                P.op(ACT, lambda e, t=t, n=n: e.activation(out=TMP[t][:, 0:n], in_=TMP[t][:, 0:n], func=AF.Silu),
                     reads=[b_TMP[t]], writes=[b_TMP[t]])
            for (m, g) in work:
                n = ncols(g)
                t = ts[(m, g)]
                P.op(DVE, lambda e, t=t, pu=du[g][m], n=n: e.tensor_tensor(
                    out=TMP[t][:, 0:n], in0=pu.ap(), in1=TMP[t][:, 0:n], op=ALU.mult),
                    reads=[du[g][m].buf, b_TMP[t]], writes=[b_TMP[t]])
            for (m, g) in work:
                lo, hi = cols(g)
                n = hi - lo
                t = ts[(m, g)]
                j = 2 * b + m
                P.op(DVE, lambda e, t=t, j=j, lo=lo, hi=hi, n=n: e.tensor_tensor(
                    out=ACTB[:, j, lo:hi], in0=TMP[t][:, 0:n], in1=RS[:, lo:hi], op=ALU.mult),
                    reads=[b_TMP[t], b_RS[g]], writes=(bA_lo if j < 14 else bA_hi))
            if b == 0 and after_norm is not None:
                after_norm()

        def rhs_A(k, g):
            lo, hi = cols(g)
            return ACTB[:, k, lo:hi]

        A_bufs = {"m": b_ACT, "h": b_ACT}
        xs = XStats(next_groups, from_xf=final)
        xn = XStats(["m"], alt=True, hp_cg=C_N1) if next_load is not None else None
        xn_fed = 0
        for cb in range(NKD // 2):
            dy = proj(w_out, NKF, cb * 256, rhs_A, A_bufs, groups)
            for m in range(2):
                j = 2 * cb + m
                for g in groups:
                    lo, hi = cols(g)
                    if final:
                        P.op(DVE, lambda e, py=dy[g][m], j=j: e.scalar_tensor_tensor(
                            out=XF(j), in0=py.ap(), scalar=0.5, in1=X[:, j, 0:T], op0=ALU.mult, op1=ALU.add),
                            reads=[dy[g][m].buf, b_X[g][j]], writes=xf_bufs(j))
                        continue
                    P.op(DVE, lambda e, py=dy[g][m], j=j, lo=lo, hi=hi: e.scalar_tensor_tensor(
                        out=X[:, j, lo:hi], in0=py.ap(), scalar=0.5, in1=X[:, j, lo:hi],
                        op0=ALU.mult, op1=ALU.add),
                        reads=[dy[g][m].buf, b_X[g][j]], writes=[b_X[g][j]])
            xs.feed([2 * cb, 2 * cb + 1])
            if xn is not None:
                if cb % 2 == 1:
                    next_load((cb - 1) // 2)
                if cb >= 3:
                    xn.feed([xn_fed, xn_fed + 1])
                    xn_fed += 2
        dfin = xs.finish()
        if xn is None:
            return dfin
        while xn_fed < NKD:
            xn.feed([xn_fed, xn_fed + 1])
            xn_fed += 2
        return dfin, xn

    def mixer(ti, groups, dst):
        first = (ti == 0)
        rmsnorm_to_H(C_N2, groups, dst)
        if not first:
            P.op(DVE, lambda e: e.tensor_copy(out=G[:, :, 0:HALO], in_=GH[:]), reads=[b_GH], writes=b_G)
            P.op(ACT, lambda e: e.copy(out=U[:, :, 0:16], in_=UH[:]), reads=[b_UH], writes=b_U)
        for b in range(4):
            du = proj(dr["w_in"], NKD, b * 256, rhs_H, H_bufs, groups)
            for m in range(2):
                j = 2 * b + m
                for g in groups:
                    if g == "m":
                        P.op(ACT, lambda e, pu=du[g][m], j=j: e.activation(
                            out=U[:, j, 16:16 + T], in_=pu.ap(), func=AF.Identity, bias=cst(C_BIN + j)),
                            reads=[du[g][m].buf, b_CST], writes=[b_U[j]])
                    else:
                        P.op(ACT, lambda e, pu=du[g][m], j=j: e.activation(
                            out=U[:, j, 0:16], in_=pu.ap(16, 32), func=AF.Identity, bias=cst(C_BIN + j)),
                            reads=[du[g][m].buf, b_CST], writes=[b_U[j]])
                        P.op(ACT, lambda e, j=j: e.activation(out=U[:, j, 0:16], in_=U[:, j, 0:16], func=AF.Copy,
                                                              scale=cst(C_MASK)),
                             reads=[b_U[j], b_CST], writes=[b_U[j]])
        P.op(ACT, lambda e: e.copy(out=UH[:], in_=U[:, :, T:T + 16]), reads=b_U, writes=[b_UH])
        WU = T + 16
        for gi in range(4):
            w = 2 << gi
            j0 = 2 * gi
            ub = [b_U[j0], b_U[j0 + 1]]
            src = U[:, j0:j0 + 2, :]
            src_b = ub
            sh = 1
            off = 0
            for step in range(gi + 1):
                p = npt()
                off2 = off + sh
                P.op(DVE, lambda e, p=p, src=src, off=off, off2=off2, sh=sh: e.tensor_tensor(
                    out=PT[p][:, :, off2:WU], in0=src[:, :, off2:WU], in1=src[:, :, off:WU - sh], op=ALU.add),
                    reads=src_b, writes=[b_PT[p]])
                src = PT[p]
                src_b = [b_PT[p]]
                off = off2
                sh *= 2
            if first:
                for m in range(2):
                    P.op(DVE, lambda e, src=src, m=m, gi=gi: e.tensor_tensor(
                        out=src[:, m, 16:16 + HALO], in0=src[:, m, 16:16 + HALO],
                        in1=CST[:, C_INVC + gi * HALO:C_INVC + (gi + 1) * HALO], op=ALU.mult),
                        reads=src_b + [b_CST], writes=src_b)
                P.op(DVE, lambda e, src=src, j0=j0: e.tensor_tensor(
                    out=PLD[:, j0:j0 + 2, 0:HALO], in0=src[:, :, 16:16 + HALO], in1=U[:, j0:j0 + 2, 16:16 + HALO],
                    op=ALU.subtract), reads=src_b + ub, writes=[b_PC[j0], b_PC[j0 + 1]])
                lo = HALO
            else:
                lo = 0
            P.op(DVE, lambda e, src=src, j0=j0, w=w, lo=lo: e.scalar_tensor_tensor(
                out=PLD[:, j0:j0 + 2, lo:T], in0=src[:, :, 16 + lo:16 + T], scalar=1.0 / w,
                in1=U[:, j0:j0 + 2, 16 + lo:16 + T], op0=ALU.mult, op1=ALU.subtract),
                reads=src_b + ub, writes=[b_PC[j0], b_PC[j0 + 1]])
        for b in range(4):
            da = proj(dr["w_in"], NKD, POOLW + b * 256, rhs_H, H_bufs, groups)
            dg = proj(dr["w_in"], NKD, POOLW + CONVW + b * 256, rhs_H, H_bufs, groups)
            for m in range(2):
                j = 2 * b + m
                for g in groups:
                    n = ncols(g)
                    t = ntmp()
                    P.op(ACT, lambda e, t=t, pg=dg[g][m], n=n, j=j: e.activation(
                        out=TMP[t][:, 0:n], in_=pg.ap(), func=AF.Sigmoid, bias=cst(C_BIN + 16 + j)),
                        reads=[dg[g][m].buf, b_CST], writes=[b_TMP[t]])
                    dst = G[:, j, HALO:WCOL] if g == "m" else G[:, j, 0:HALO]
                    P.op(DVE, lambda e, t=t, pa=da[g][m], n=n, j=j, dst=dst: e.scalar_tensor_tensor(
                        out=dst, in0=pa.ap(), scalar=cst(C_BIN + 8 + j), in1=TMP[t][:, 0:n],
                        op0=ALU.add, op1=ALU.mult),
                        reads=[da[g][m].buf, b_TMP[t], b_CST], writes=[b_G[j]])
                    if g == "h":
                        P.op(DVE, lambda e, dst=dst: e.tensor_scalar(out=dst, in0=dst, scalar1=cst(C_MASK), scalar2=None,
                                                                     op0=ALU.mult),
                             reads=[b_G[j], b_CST], writes=[b_G[j]])
        P.op(DVE, lambda e: e.tensor_copy(out=GH[:], in_=G[:, :, T:WCOL]), reads=b_G, writes=[b_GH])
        for gi in range(4):
            dmix = proj_grp(gi)
            for m in range(2):
                j = 2 * gi + m
                P.op(ACT, lambda e, pm=dmix[m], j=j: e.activation(out=MX[:, j, :], in_=pm.ap(), func=AF.Copy,
                                                                   scale=cst(C_PSC + j)),
                     reads=[dmix[m].buf, b_CST], writes=[b_MX[j]])
        MXb = {"m": b_MX}
        CNb = {"m": b_PC}
        Hm = {"m": b_H["m"]}
        def conv_op(n):
            i, j = divmod(n, 8)
            if i == 0:
                P.op(DVE, lambda e, j=j: e.tensor_scalar(
                    out=C[:, j, :], in0=G[:, j, 2:2 + T], scalar1=cst(C_CVK + j * KCONV), scalar2=cst(C_CVB + j),
                    op0=ALU.mult, op1=ALU.add), reads=[b_G[j], b_CST], writes=[b_C[j]])
            else:
                P.op(DVE, lambda e, j=j, i=i: e.scalar_tensor_tensor(
                    out=C[:, j, :], in0=G[:, j, 2 + i:2 + i + T], scalar=cst(C_CVK + j * KCONV + i), in1=C[:, j, :],
                    op0=ALU.mult, op1=ALU.add), reads=[b_G[j], b_C[j], b_CST], writes=[b_C[j]])

        for cb in range(8):
            dga = proj(dr["w_in"], NKD, POOLW + 2 * CONVW + cb * 256, rhs_H, Hm, ["m"])
            da = proj(dr["pool_w_proj"], 8, cb * 256, lambda k, g: MX[:, k, :], MXb, ["m"])
            ts = []
            for m in range(2):
                j = 2 * cb + m
                t = ntmp()
                ts.append(t)
                P.op(ACT, lambda e, t=t, pg=dga["m"][m], j=j: e.activation(
                    out=TMP[t][:, 0:T], in_=pg.ap(), func=AF.Sigmoid, bias=cst(C_BIN + 24 + j)),
                    reads=[dga["m"][m].buf, b_CST], writes=[b_TMP[t]])
            dgb = proj(dr["w_in"], NKD, POOLW + 2 * CONVW + D + cb * 256, rhs_H, Hm, ["m"])
            for m in range(2):
                j = 2 * cb + m
                P.op(ACT, lambda e, pg=dgb["m"][m], j=j: e.activation(
                    out=SGB[:, j, :], in_=pg.ap(), func=AF.Sigmoid, bias=cst(C_BIN + 40 + j)),
                    reads=[dgb["m"][m].buf, b_CST], writes=[b_SGB[j]])
            for n in range(cb * KCONV, (cb + 1) * KCONV):
                conv_op(n)
            for m in range(2):
                j = 2 * cb + m
                t = ts[m]
                P.op(DVE, lambda e, t=t, p_=da["m"][m], j=j: e.tensor_tensor(
                    out=MERGED[:, j, :], in0=p_.ap(), in1=TMP[t][:, 0:T], op=ALU.mult),
                    reads=[da["m"][m].buf, b_TMP[t]], writes=[b_M[j]] + b_U)
        d = PS("m")
        for j in range(8):
            P.op(PE, lambda e, j=j: e.matmul(d.ap(), ONES[:], C[:, j, :], start=(j == 0), stop=(j == 7)),
                 reads=[b_C[j], b_ONES], writes=[d.buf])
        P.op(ACT, lambda e: e.activation(out=MEAN[:], in_=d.ap(), func=AF.Copy, scale=1.0 / CONVW),
             reads=[d.buf], writes=[b_MEAN])
        for j in range(8):
            P.op(DVE, lambda e, j=j: e.tensor_tensor(out=C[:, j, :], in0=C[:, j, :], in1=MEAN[:], op=ALU.subtract),
                 reads=[b_C[j], b_MEAN], writes=[b_C[j]])
        d2 = stats(lambda j, g: C[:, j, :], lambda j, g: b_C[j], 8, "m", direct=True)
        rstd_from(d2, CONVW, "m", RS[:, 0:T], b_RS["m"])
        for j in range(8):
            P.op(DVE, lambda e, j=j: e.tensor_tensor(out=C[:, j, :], in0=C[:, j, :], in1=RS[:, 0:T], op=ALU.mult),
                 reads=[b_C[j], b_RS["m"]], writes=[b_C[j]])
            P.op(ACT, lambda e, j=j: e.activation(out=CN[:, j, :], in_=C[:, j, :], func=AF.Silu,
                                                  bias=cst(C_LNB + j), scale=cst(C_LNG + j)),
                 reads=[b_C[j], b_CST], writes=[b_PC[j]])
        for cb in range(8):
            dbb = proj(dr["conv_w_proj"], 8, cb * 256, lambda k, g: CN[:, k, :], CNb, ["m"])
            for m in range(2):
                j = 2 * cb + m
                t = ntmp()
                P.op(DVE, lambda e, t=t, p_=dbb["m"][m], j=j: e.tensor_tensor(
                    out=TMP[t][:, 0:T], in0=p_.ap(), in1=SGB[:, j, :], op=ALU.mult),
                    reads=[dbb["m"][m].buf, b_SGB[j]], writes=[b_TMP[t]])
                P.op(DVE, lambda e, t=t, j=j: e.tensor_tensor(
                    out=MERGED[:, j, :], in0=TMP[t][:, 0:T], in1=MERGED[:, j, :], op=ALU.add),
                    reads=[b_TMP[t], b_M[j]], writes=[b_M[j]])
        xs = XStats(["m"], hp_cg=C_N3)
        for cb in range(8):
            dy = proj(dr["w_out"], NKD, cb * 256, lambda k, g: MERGED[:, k, :], {"m": b_M + b_U}, ["m"])
            for m in range(2):
                j = 2 * cb + m
                P.op(DVE, lambda e, py=dy["m"][m], j=j: e.tensor_tensor(
                    out=X[:, j, 0:T], in0=py.ap(), in1=X[:, j, 0:T], op=ALU.add),
                    reads=[dy["m"][m].buf, b_X["m"][j]], writes=[b_X["m"][j]])
            xs.feed([2 * cb, 2 * cb + 1])
        return xs

    def proj_grp(gi):
        j0 = 2 * gi
        wt = dr["pool_w_grp"]
        d = proj(wt[gi * 256:(gi + 1) * 256, :], 2, 0,
                 lambda k, g: PLD[:, j0 + k, :], {"m": [b_PC[j0], b_PC[j0 + 1]]}, ["m"])
        return d["m"]

    xT = dr["xT"]
    outT = dr["outT"]
    def load_x_group(ti, q):
        c0 = HALO + ti * T
        src = xT[q * 512:(q + 1) * 512, c0:c0 + T].rearrange("(kc p) t -> p kc t", p=128)
        P.op(SP, lambda e, src=src, q=q: e.dma_start(out=X[:, 4 * q:4 * q + 4, 0:T], in_=src),
             writes=b_X["m"][4 * q:4 * q + 4], semkey=s_xq[q], incval=16)

    def finalize(ti, d3):
        rstd_from(d3["m"], D, "m", RS2[:], b_RS2)
        for j in range(NKD):
            P.op(DVE, lambda e, j=j: e.scalar_tensor_tensor(
                out=OUTB[:, j, :], in0=XF(j), scalar=cst(C_NF + j), in1=RS2[:],
                op0=ALU.mult, op1=ALU.mult), reads=xf_bufs(j) + [b_RS2, b_CST], writes=bA_hi)
        dsto = outT[:, ti * T:(ti + 1) * T].rearrange("(kc p) t -> p kc t", p=128)
        P.op(SP, lambda e, dsto=dsto: e.dma_start(out=dsto, in_=OUTB[:]), reads=bA_hi, writes=[b_OUT], semkey=s_o, incval=16)

    for q in range(4):
        load_x_group(0, q)
    srch = xT[:, 0:HALO].rearrange("(kc p) t -> p kc t", p=128)
    P.op(SP, lambda e: e.dma_start(out=X[:, :, T:WCOL], in_=srch), writes=b_X["h"], semkey=s_xh, incval=16)
    xs0 = XStats(["m", "h"], hp_cg=C_N1)
    for j in range(NKD):
        xs0.feed([j])
    d0 = xs0
    pending = None
    for ti in range(n_tiles):
        groups = ["m", "h"] if ti == 0 else ["m"]
        d1 = ffn(C_N1, dr["ffn1_w_in"], dr["ffn1_w_out"], groups, d0, groups, after_norm=pending)
        d2 = mixer(ti, groups, d1)
        if ti + 1 < n_tiles:
            d3, d0 = ffn(C_N3, dr["ffn2_w_in"], dr["ffn2_w_out"], ["m"], d2, ["m"], final=True,
                         next_load=lambda q, ti=ti: load_x_group(ti + 1, q))
            pending = (lambda ti=ti, d3=d3: finalize(ti, d3))
        else:
            d3 = ffn(C_N3, dr["ffn2_w_in"], dr["ffn2_w_out"], ["m"], d2, ["m"], final=True)
            finalize(ti, d3)
    P.wait_for(SP, [b_OUT])
    P.emit_all()
    return P


def build_nc(n_tiles=4):
    nc = bass.Bass("TRN2", target_bir_lowering=False)
    ntok = n_tiles * T
    dr = {}

    def din(name, shape):
        dr[name] = nc.dram_tensor(name, list(shape), F32, kind="ExternalInput").ap()

    din("xT", (D, HALO + ntok))
    din("cst", (128, NCST))
    din("ffn1_w_in", (D, 2 * DFF))
    din("ffn1_w_out", (DFF, D))
    din("w_in", (D, INW))
    din("pool_w_grp", (POOLW, 256))
    din("pool_w_proj", (POOLW, D))
    din("conv_w_proj", (CONVW, D))
    din("w_out", (D, D))
    din("ffn2_w_in", (D, 2 * DFF))
    din("ffn2_w_out", (DFF, D))
    dr["outT"] = nc.dram_tensor("outT", [D, ntok], F32, kind="ExternalOutput").ap()
    with ExitStack() as st:
        emit_program(nc, st, dr, n_tiles)
    return nc


def pack_consts(inp, seq_start):
    c = np.zeros((128, NCST), np.float32)

    def put(col, vec):
        n = vec.shape[0] // 128
        c[:, col:col + n] = vec.reshape(n, 128).T

    put(C_N1, inp["ffn1_norm"].reshape(-1))
    put(C_N2, inp["mix_norm"].reshape(-1))
    put(C_N3, inp["ffn2_norm"].reshape(-1))
    put(C_NF, inp["final_norm"].reshape(-1))
    put(C_BIN, inp["b_in"].reshape(-1))
    put(C_PSC, inp["pool_scale"].reshape(-1))
    put(C_CVB, inp["conv_b"].reshape(-1))
    put(C_LNG, inp["conv_ln_g"].reshape(-1))
    put(C_LNB, inp["conv_ln_b"].reshape(-1))
    k = inp["conv_dw"].reshape(KCONV, CONVW)
    c[:, C_CVK:C_CVK + 8 * KCONV] = k.reshape(KCONV, 8, 128).transpose(2, 1, 0).reshape(128, 8 * KCONV)
    c[:, C_MASK] = 0.0 if seq_start else 1.0
    c[:, C_EPS] = EPS
    for gi in range(4):
        w = 2 << gi
        for t in range(HALO):
            cnt = min(t + 1, w) if seq_start else w
            c[:, C_INVC + gi * HALO + t] = 1.0 / cnt
    return c


def make_in_maps(inp, n_cores=8, n_tiles=4):
    x = np.asarray(inp["x"], np.float32)
    shared = {
        "ffn1_w_in": np.ascontiguousarray(np.asarray(inp["ffn1_w_in"], np.float32).reshape(D, 2 * DFF)),
        "ffn1_w_out": np.ascontiguousarray(np.asarray(inp["ffn1_w_out"], np.float32).reshape(DFF, D)),
        "w_in": np.ascontiguousarray(np.asarray(inp["w_in"], np.float32).reshape(D, INW)),
        "pool_w_grp": np.ascontiguousarray(np.asarray(inp["pool_w_grp"], np.float32).reshape(POOLW, 256)),
        "pool_w_proj": np.ascontiguousarray(np.asarray(inp["pool_w_proj"], np.float32).reshape(POOLW, D)),
        "conv_w_proj": np.ascontiguousarray(np.asarray(inp["conv_w_proj"], np.float32).reshape(CONVW, D)),
        "w_out": np.ascontiguousarray(np.asarray(inp["w_out"], np.float32).reshape(D, D)),
        "ffn2_w_in": np.ascontiguousarray(np.asarray(inp["ffn2_w_in"], np.float32).reshape(D, 2 * DFF)),
        "ffn2_w_out": np.ascontiguousarray(np.asarray(inp["ffn2_w_out"], np.float32).reshape(DFF, D)),
    }
    small = {k: np.asarray(v, np.float32) for k, v in inp.items()
             if k in ("ffn1_norm", "mix_norm", "ffn2_norm", "final_norm", "b_in", "pool_scale", "conv_b",
                      "conv_ln_g", "conv_ln_b", "conv_dw")}
    cst = {True: pack_consts(small, True), False: pack_consts(small, False)}
    ntok = n_tiles * T
    maps = []
    for c in range(n_cores):
        b, hf = divmod(c, 2)
        t0 = hf * TOK_CORE
        xt = np.zeros((D, HALO + ntok), np.float32)
        xt[:, HALO:] = x[b, t0:t0 + ntok, :].T
        if t0 > 0:
            xt[:, :HALO] = x[b, t0 - HALO:t0, :].T
        m = {"xT": xt, "cst": cst[t0 == 0]}
        m.update(shared)
        maps.append(m)
    return maps


_NC_CACHE = {}


def kernel(**inputs):
    if "nc" not in _NC_CACHE:
        _NC_CACHE["nc"] = build_nc(4)
    nc = _NC_CACHE["nc"]
    maps = make_in_maps(inputs, 8, 4)
    res = run_bass_kernel_spmd(nc, maps, core_ids=list(range(8)))
    out = np.empty((BATCH, SEQ, D), np.float32)
    for c in range(8):
        b, hf = divmod(c, 2)
        out[b, hf * TOK_CORE:(hf + 1) * TOK_CORE, :] = res.results[c]["outT"].T
    return out
```

```python
from contextlib import ExitStack
import numpy as np
import concourse.bass as bass
import concourse.mybir as mybir
from concourse.bass_utils import run_bass_kernel_spmd

F32 = mybir.dt.float32
BF16 = mybir.dt.bfloat16
AF = mybir.ActivationFunctionType
ALU = mybir.AluOpType

D = 2048
DFF = 5632
NKD = D // 128
NKF = DFF // 128
SEQ = 4096
BATCH = 4
TOK_CORE = 2048
T = 512
HALO = 32
WCOL = T + HALO
POOLW = 1024
CONVW = 1024
KCONV = 31
INW = POOLW + 2 * CONVW + 2 * D
EPS = 1e-6
RING = 5
NBANK = 8

PE, ACT, DVE, POOL, SP = "tensor", "scalar", "vector", "gpsimd", "sync"
ENGS = (PE, ACT, DVE, POOL, SP)

C_N1 = 0
C_N2 = C_N1 + NKD
C_N3 = C_N2 + NKD
C_NF = C_N3 + NKD
C_BIN = C_NF + NKD
C_PSC = C_BIN + INW // 128
C_CVB = C_PSC + 8
C_LNG = C_CVB + 8
C_LNB = C_LNG + 8
C_CVK = C_LNB + 8
C_MASK = C_CVK + 8 * KCONV
C_EPS = C_MASK + 1
C_INVC = C_EPS + 1
NCST = C_INVC + 4 * HALO


class Buf:
    __slots__ = ("name", "w", "r")

    def __init__(self, name):
        self.name = name
        self.w = None
        self.r = []


class Prog:
    def __init__(self, nc, stack, dry=False):
        self.nc = nc
        self.stack = stack
        self.dry = dry
        self.q = {e: [] for e in ENGS}
        self.sems = {}
        self.cnt = {}
        self.seen = {e: {} for e in ENGS}
        for e in ENGS:
            self.newsem(e)
        self.n_ops = 0

    def newsem(self, key):
        if not self.dry:
            self.sems[key] = self.stack.enter_context(self.nc.semaphore("s_" + str(key)))
        self.cnt[key] = 0
        return key

    def _deps(self, eng, reads, writes):
        need = {}

        def add(tok, raw):
            if tok is None:
                return
            key, val, teng = tok
            if teng == eng and key == eng and eng == PE:
                return
            if need.get(key, 0) < val:
                need[key] = val

        for b in reads:
            add(b.w, True)
        for b in writes:
            add(b.w, False)
            for t in b.r:
                add(t, False)
        out = []
        for key, val in need.items():
            if self.seen[eng].get(key, 0) < val:
                self.seen[eng][key] = val
                out.append((key, val))
        return out

    def op(self, eng, fn, reads=(), writes=(), semkey=None, incval=1):
        if self.dry:
            return None
        waits = self._deps(eng, reads, writes)
        key = semkey if semkey is not None else eng
        self.cnt[key] += incval
        tok = (key, self.cnt[key], eng)
        sems = self.sems
        sem = sems[key]

        def emit(e, waits=waits, fn=fn, sem=sem, incval=incval):
            for k, v in waits:
                e.wait_ge(sems[k], v)
            fn(e).then_inc(sem, incval)

        self.q[eng].append(emit)
        self.n_ops += 1
        for b in writes:
            b.w = tok
            b.r = []
        for b in reads:
            b.r.append(tok)
        return tok

    def wait_for(self, eng, bufs):
        if self.dry:
            return
        waits = self._deps(eng, bufs, ())
        sems = self.sems

        def emit(e, waits=waits):
            for k, v in waits:
                e.wait_ge(sems[k], v)

        self.q[eng].append(emit)

    def emit_all(self):
        with self.nc.Block() as block:
            for ename in ENGS:
                lst = self.q[ename]
                if not lst:
                    continue

                def body(e, lst=lst):
                    for f in lst:
                        f(e)

                getattr(block, ename)(body)


class WStream:
    def __init__(self, P, ring):
        self.P = P
        self.ring = ring
        self.n = 0
        self.bufs = [Buf(f"ring{i}") for i in range(RING)]
        self.semk = [P.newsem(f"ring{i}") for i in range(RING)]

    def get(self, wt, k0, nk, c0):
        n = self.n
        self.n += 1
        s = n % RING
        src = wt[k0 * 128:(k0 + nk) * 128, c0:c0 + 256].rearrange("(kc p) c -> p kc c", p=128)
        dst = self.ring[s]
        self.P.op(POOL, lambda e, dst=dst, src=src, nk=nk: e.dma_start(out=dst[:, 0:nk, :], in_=src),
                  writes=[self.bufs[s]], semkey=self.semk[s], incval=16)
        return s


def emit_program(nc, st, dr, n_tiles):
    dry = False
    P = Prog(nc, st)

    def sb(name, shape, dt):
        if dry:
            return None
        return st.enter_context(nc.sbuf_tensor(name, shape, dt))

    X = sb("X", [128, NKD, WCOL], F32)
    H = sb("H", [128, NKD, WCOL], BF16)
    SCRW = 16768
    SCR = sb("SCR", [128, SCRW], F32)
    CST = sb("CST", [128, NCST], F32)
    ONES = sb("ONES", [128, 128], F32)
    NTMP = 6
    TMP = [sb(f"TMP{i}", [128, WCOL], F32) for i in range(NTMP)]
    PT = [sb(f"PT{i}", [128, 2, T + 16], F32) for i in range(2)]
    RS = sb("RS", [128, WCOL], F32)
    ACC = {"m": sb("ACCm", [128, T], F32), "h": sb("ACCh", [128, HALO], F32)}
    ACC2 = sb("ACC2", [128, T], F32)
    MEAN = sb("MEAN", [128, T], F32)
    GH = sb("GH", [128, 8, HALO], F32)
    SGBT = sb("SGBT", [128, NKD * T], BF16)
    RS2 = sb("RS2", [128, T], F32)
    UH = sb("UH", [128, 8, 16], F32)
    ring = [sb(f"ring{i}", [128, 16, 256], BF16) for i in range(RING)]
    if not dry:
        banks = [st.enter_context(nc.psum_tensor(f"bank{i}", [128, 512], F32)) for i in range(8)]
        ACTB = SCR[:, 0:NKF * WCOL // 2].bitcast(BF16).rearrange("p (c w) -> p c w", w=WCOL)
        o = 0
        G = SCR[:, o:o + 8 * WCOL].rearrange("p (c w) -> p c w", w=WCOL)
        OUTB = SCR[:, 3808:3808 + NKD * T].rearrange("p (c w) -> p c w", w=T)
        o += 8 * WCOL
        C = SCR[:, o:o + 8 * T].rearrange("p (c w) -> p c w", w=T)
        o += 8 * T
        U = SCR[:, o:o + 8 * (T + 16)].rearrange("p (c w) -> p c w", w=T + 16)
        MERGED = SCR[:, o:o + 8 * T].bitcast(BF16).rearrange("p (c w) -> p c w", w=T)
        o += 8 * (T + 16)
        PLD = SCR[:, o:o + 4 * T].bitcast(BF16).rearrange("p (c w) -> p c w", w=T)
        CN = PLD
        o += 4 * T
        MX = SCR[:, o:o + 4 * T].bitcast(BF16).rearrange("p (c w) -> p c w", w=T)
        o += 4 * T
        assert o <= SCRW and NKF * WCOL // 2 <= SCRW
        SGB = SGBT[:].rearrange("p (c w) -> p c w", w=T)
        XF_lo = SGBT[:].bitcast(F32).rearrange("p (c w) -> p c w", w=T)
        XF_hi = SCR[:, 12000:12000 + 8 * T].rearrange("p (c w) -> p c w", w=T)
    W = WStream(P, ring)

    b_X = {"m": [Buf(f"Xm{j}") for j in range(NKD)], "h": [Buf(f"Xh{j}") for j in range(NKD)]}
    b_H = {"m": [Buf(f"Hm{j}") for j in range(NKD)], "h": [Buf(f"Hh{j}") for j in range(NKD)]}
    b_G = [Buf(f"G{j}") for j in range(8)]
    b_C = [Buf(f"C{j}") for j in range(8)]
    b_U = [Buf(f"U{j}") for j in range(8)]
    b_PC = [Buf(f"PC{j}") for j in range(8)]
    b_MX = [Buf(f"MX{j}") for j in range(8)]
    b_ACT = b_G + b_C + b_U
    bA_lo = b_G[0:7]
    bA_hi = [b_G[7]] + b_C + b_U[0:7]
    b_CST = Buf("CST")
    b_ONES = Buf("ONES")
    b_TMP = [Buf(f"TMP{i}") for i in range(NTMP)]
    b_PT = [Buf(f"PT{i}") for i in range(2)]
    b_RS = {"m": Buf("RSm"), "h": Buf("RSh")}
    b_RS2 = Buf("RS2")
    b_MEAN = Buf("MEAN")
    b_ACC = {"m": Buf("ACCm"), "h": Buf("ACCh")}
    b_ACC2 = Buf("ACC2")
    b_M = [Buf(f"M{j}") for j in range(NKD)]
    b_GH = Buf("GH")
    b_SGB = [Buf(f"SGB{j}") for j in range(NKD)]
    b_UH = Buf("UH")
    b_bank = [Buf(f"bank{i}") for i in range(NBANK)]
    b_hs = [Buf(f"hs{i}") for i in range(16)]
    s_xq = [P.newsem(f"xload{q}") for q in range(4)]
    s_c = P.newsem("cload")
    s_xh = P.newsem("xhload")
    s_oq = [P.newsem(f"ostore{q}") for q in range(4)]
    b_OUTq = [Buf(f"OUTDRAM{q}") for q in range(4)]
    b_OB = [Buf(f"OUTB{q}") for q in range(4)]

    state = {"bank": 0, "hs": 0, "tmp": 0, "pt": 0, "stream": False}

    def nbank():
        i = state["bank"]
        state["bank"] = (i + 1) % NBANK
        return i

    def nhs():
        i = state["hs"]
        state["hs"] = (i + 1) % 16
        return i

    def ntmp():
        i = state["tmp"]
        state["tmp"] = (i + 1) % NTMP
        return i

    def npt():
        i = state["pt"]
        state["pt"] = (i + 1) % 2
        return i

    def cols(g):
        return (0, T) if g == "m" else (T, WCOL)

    def ncols(g):
        return T if g == "m" else HALO

    def cst(c):
        return CST[:, c:c + 1]

    class PS:
        def __init__(self, g, full=False):
            self.g = g
            self.n = ncols(g)
            if True:
                self.i = nbank()
                self.buf = b_bank[self.i]
                self.base = banks[self.i]
                self.o = 0
            else:
                self.i = nhs()
                self.buf = b_hs[self.i]
                self.base = banks[7]
                self.o = self.i * HALO

        def ap(self, lo=0, hi=None):
            hi = self.n if hi is None else hi
            return self.base[:, self.o + lo:self.o + hi]

    P.op(SP, lambda e: e.dma_start(out=CST[:], in_=dr["cst"]), writes=[b_CST], semkey=s_c, incval=16)
    P.op(DVE, lambda e: e.memset(ONES[:], 1.0), writes=[b_ONES])

    def proj(wt, nkc, c0, rhs, rhs_bufs, groups):
        full = nkc > 16
        dst = {g: [PS(g, full), PS(g, full)] for g in groups}
        stream = state["stream"] and rhs is rhs_H
        if stream:
            state["stream"] = False
        for kb in range(0, nkc, 16):
            nk = min(16, nkc - kb)
            s = W.get(wt, kb, nk, c0)
            allr = [W.bufs[s]] + [b for g in groups for b in rhs_bufs[g]]
            if stream:
                d0 = dst["m"][0]
                for kc in range(nk):
                    P.op(PE, lambda e, s=s, kc=kc, d0=d0: e.matmul(
                        d0.ap(), ring[s][:, kc, 0:128], rhs(kc, "m"), start=(kc == 0), stop=(kc == nkc - 1)),
                        reads=[W.bufs[s], rhs_bufs["m"][kc]], writes=[d0.buf])

            def mm(e, s=s, kb=kb, nk=nk, stream=stream):
                ins = None
                for m in range(2):
                    for g in groups:
                        if stream and m == 0 and g == "m":
                            continue
                        d = dst[g][m].ap()
                        for kc in range(nk):
                            k = kb + kc
                            ins = e.matmul(d, ring[s][:, kc, m * 128:(m + 1) * 128], rhs(k, g),
                                           start=(k == 0), stop=(k == nkc - 1))
                return ins

            P.op(PE, mm, reads=allr, writes=[dst[g][m].buf for g in groups for m in range(2)])
        return dst

    def stats(src, src_bufs, nch, g, direct=False):
        d = PS(g)
        n = ncols(g)
        if direct:
            for j in range(nch):
                t = ntmp()
                P.op(ACT, lambda e, t=t, j=j: e.activation(out=TMP[t][:, 0:n], in_=src(j, g), func=AF.Square),
                     reads=[src_bufs(j, g)], writes=[b_TMP[t]])
                P.op(PE, lambda e, t=t, j=j: e.matmul(d.ap(), ONES[:], TMP[t][:, 0:n],
                                                       start=(j == 0), stop=(j == nch - 1)),
                     reads=[b_TMP[t], b_ONES], writes=[d.buf])
            return d
        acc = ACC[g]
        t0 = None
        for j in range(nch):
            t = ntmp()
            P.op(ACT, lambda e, t=t, j=j: e.activation(out=TMP[t][:, 0:n], in_=src(j, g), func=AF.Square),
                 reads=[src_bufs(j, g)], writes=[b_TMP[t]])
            if j == 0:
                t0 = t
            elif j == 1:
                P.op(DVE, lambda e, t=t, t0=t0: e.tensor_tensor(out=acc[:, 0:n], in0=TMP[t0][:, 0:n], in1=TMP[t][:, 0:n],
                                                                op=ALU.add),
                     reads=[b_TMP[t0], b_TMP[t]], writes=[b_ACC[g]])
            else:
                P.op(DVE, lambda e, t=t: e.tensor_tensor(out=acc[:, 0:n], in0=acc[:, 0:n], in1=TMP[t][:, 0:n], op=ALU.add),
                     reads=[b_ACC[g], b_TMP[t]], writes=[b_ACC[g]])
        P.op(PE, lambda e: e.matmul(d.ap(), ONES[:], acc[:, 0:n], start=True, stop=True),
             reads=[b_ACC[g], b_ONES], writes=[d.buf])
        return d

    def rstd_from(d, n, g, dst_ap, dst_buf):
        P.op(ACT, lambda e: e.activation(out=dst_ap, in_=d.ap(), func=AF.Sqrt, bias=cst(C_EPS), scale=1.0 / n),
             reads=[d.buf, b_CST], writes=[dst_buf])
        P.op(DVE, lambda e: e.reciprocal(out=dst_ap, in_=dst_ap), reads=[dst_buf], writes=[dst_buf])

    def XF(j):
        return XF_lo[:, j, :] if j < 8 else XF_hi[:, j - 8, :]

    def xf_bufs(j):
        if j < 8:
            return [b_SGB[2 * j], b_SGB[2 * j + 1]]
        return [b_U[6], b_U[7]] + b_PC + b_MX + b_M

    class XStats:
        def __init__(self, groups, from_xf=False, alt=False, hp_cg=None):
            self.groups = groups
            self.from_xf = from_xf
            self.hp_cg = hp_cg
            self.acc = {"m": ACC2, "h": None} if alt else ACC
            self.accb = {"m": b_ACC2, "h": None} if alt else b_ACC
            self.pend = {g: [] for g in groups}
            self.n = {g: 0 for g in groups}
            self.first = {g: None for g in groups}

        def flush(self):
            for g in self.groups:
                n = ncols(g)
                acc = self.acc[g]
                b_acc = self.accb[g]
                for t in self.pend[g]:
                    k = self.n[g]
                    self.n[g] += 1
                    if k == 0:
                        self.first[g] = t
                    elif k == 1:
                        t0 = self.first[g]
                        P.op(DVE, lambda e, t=t, t0=t0, acc=acc, n=n: e.tensor_tensor(
                            out=acc[:, 0:n], in0=TMP[t0][:, 0:n], in1=TMP[t][:, 0:n], op=ALU.add),
                            reads=[b_TMP[t0], b_TMP[t]], writes=[b_acc])
                    else:
                        P.op(DVE, lambda e, t=t, acc=acc, n=n: e.tensor_tensor(
                            out=acc[:, 0:n], in0=acc[:, 0:n], in1=TMP[t][:, 0:n], op=ALU.add),
                            reads=[b_acc, b_TMP[t]], writes=[b_acc])
                self.pend[g] = []

        def feed(self, js):
            self.flush()
            for g in self.groups:
                lo, hi = cols(g)
                n = hi - lo
                for j in js:
                    t = ntmp()
                    if self.from_xf:
                        P.op(ACT, lambda e, t=t, j=j: e.activation(out=TMP[t][:, 0:T], in_=XF(j), func=AF.Square),
                             reads=xf_bufs(j), writes=[b_TMP[t]])
                    else:
                        P.op(ACT, lambda e, t=t, j=j, lo=lo, hi=hi, n=n: e.activation(
                            out=TMP[t][:, 0:n], in_=X[:, j, lo:hi], func=AF.Square),
                            reads=[b_X[g][j]], writes=[b_TMP[t]])
                        if self.hp_cg is not None:
                            cgj = self.hp_cg + j
                            P.op(ACT, lambda e, j=j, lo=lo, hi=hi, cgj=cgj: e.activation(
                                out=H[:, j, lo:hi], in_=X[:, j, lo:hi], func=AF.Copy, scale=cst(cgj)),
                                reads=[b_X[g][j], b_CST], writes=[b_H[g][j]])
                    self.pend[g].append(t)

        def finish(self):
            self.flush()
            out = {}
            for g in self.groups:
                n = ncols(g)
                d = PS(g)
                acc = self.acc[g]
                P.op(PE, lambda e, d=d, acc=acc, n=n: e.matmul(d.ap(), ONES[:], acc[:, 0:n], start=True, stop=True),
                     reads=[self.accb[g], b_ONES], writes=[d.buf])
                out[g] = d
            return out

    def rmsnorm_to_H(cg, groups, dst):
        for g in groups:
            lo, hi = cols(g)
            rstd_from(dst[g], D, g, RS[:, lo:hi], b_RS[g])
            for j in range(NKD):
                P.op(DVE, lambda e, j=j, lo=lo, hi=hi: e.scalar_tensor_tensor(
                    out=H[:, j, lo:hi], in0=X[:, j, lo:hi], scalar=cst(cg + j), in1=RS[:, lo:hi],
                    op0=ALU.mult, op1=ALU.mult),
                    reads=[b_X[g][j], b_RS[g], b_CST], writes=[b_H[g][j]])
        state["stream"] = False

    def rhs_H(k, g):
        lo, hi = cols(g)
        return H[:, k, lo:hi]

    H_bufs = {"m": b_H["m"], "h": b_H["h"]}

    def ffn(cg, w_in, w_out, groups, xs_in, next_groups, final=False, after_norm=None, next_load=None):
        def norm_stats():
            dst = xs_in.finish()
            for g in groups:
                lo, hi = cols(g)
                rstd_from(dst[g], D, g, RS[:, lo:hi], b_RS[g])

        early = len(groups) > 1
        if early:
            norm_stats()
        for b in range(NKF // 2):
            dg = proj(w_in, NKD, b * 256, rhs_H, H_bufs, groups)
            du = proj(w_in, NKD, DFF + b * 256, rhs_H, H_bufs, groups)
            if b == 0 and not early:
                norm_stats()
            work = [(m, g) for m in range(2) for g in groups]
            ts = {}
            for (m, g) in work:
                lo, hi = cols(g)
                n = hi - lo
                t = ntmp()
                ts[(m, g)] = t
                P.op(DVE, lambda e, t=t, pg=dg[g][m], n=n, lo=lo, hi=hi: e.tensor_tensor(
                    out=TMP[t][:, 0:n], in0=pg.ap(), in1=RS[:, lo:hi], op=ALU.mult),
                    reads=[dg[g][m].buf, b_RS[g]], writes=[b_TMP[t]])
                P.op(ACT, lambda e, t=t, n=n: e.activation(out=TMP[t][:, 0:n], in_=TMP[t][:, 0:n], func=AF.Silu),
                     reads=[b_TMP[t]], writes=[b_TMP[t]])
            for (m, g) in work:
                n = ncols(g)
                t = ts[(m, g)]
                P.op(DVE, lambda e, t=t, pu=du[g][m], n=n: e.tensor_tensor(
                    out=TMP[t][:, 0:n], in0=pu.ap(), in1=TMP[t][:, 0:n], op=ALU.mult),
                    reads=[du[g][m].buf, b_TMP[t]], writes=[b_TMP[t]])
            for (m, g) in work:
                lo, hi = cols(g)
                n = hi - lo
                t = ts[(m, g)]
                j = 2 * b + m
                P.op(DVE, lambda e, t=t, j=j, lo=lo, hi=hi, n=n: e.tensor_tensor(
                    out=ACTB[:, j, lo:hi], in0=TMP[t][:, 0:n], in1=RS[:, lo:hi], op=ALU.mult),
                    reads=[b_TMP[t], b_RS[g]], writes=(bA_lo if j < 14 else bA_hi + b_OB))
            if b == 0 and after_norm is not None:
                after_norm()

        def rhs_A(k, g):
            lo, hi = cols(g)
            return ACTB[:, k, lo:hi]

        A_bufs = {"m": b_ACT, "h": b_ACT}
        xs = XStats(next_groups, from_xf=final)
        xn = XStats(["m"], alt=True, hp_cg=C_N1) if next_load is not None else None
        xn_fed = 0
        for cb in range(NKD // 2):
            dy = proj(w_out, NKF, cb * 256, rhs_A, A_bufs, groups)
            for m in range(2):
                j = 2 * cb + m
                for g in groups:
                    lo, hi = cols(g)
                    if final:
                        P.op(DVE, lambda e, py=dy[g][m], j=j: e.scalar_tensor_tensor(
                            out=XF(j), in0=py.ap(), scalar=0.5, in1=X[:, j, 0:T], op0=ALU.mult, op1=ALU.add),
                            reads=[dy[g][m].buf, b_X[g][j]], writes=xf_bufs(j))
                        continue
                    P.op(DVE, lambda e, py=dy[g][m], j=j, lo=lo, hi=hi: e.scalar_tensor_tensor(
                        out=X[:, j, lo:hi], in0=py.ap(), scalar=0.5, in1=X[:, j, lo:hi],
                        op0=ALU.mult, op1=ALU.add),
                        reads=[dy[g][m].buf, b_X[g][j]], writes=[b_X[g][j]])
            xs.feed([2 * cb, 2 * cb + 1])
            if xn is not None:
                if cb % 2 == 1:
                    next_load((cb - 1) // 2)
                if cb >= 2:
                    xn.feed([xn_fed, xn_fed + 1])
                    xn_fed += 2
        dfin = xs.finish()
        if xn is None:
            return dfin
        while xn_fed < NKD:
            xn.feed([xn_fed, xn_fed + 1])
            xn_fed += 2
        return dfin, xn

    def mixer(ti, groups, dst):
        first = (ti == 0)
        rmsnorm_to_H(C_N2, groups, dst)
        if not first:
            P.op(DVE, lambda e: e.tensor_copy(out=G[:, :, 0:HALO], in_=GH[:]), reads=[b_GH], writes=b_G)
            P.op(ACT, lambda e: e.copy(out=U[:, :, 0:16], in_=UH[:]), reads=[b_UH], writes=b_U)
        for b in range(4):
            du = proj(dr["w_in"], NKD, b * 256, rhs_H, H_bufs, groups)
            for m in range(2):
                j = 2 * b + m
                for g in groups:
                    if g == "m":
                        P.op(ACT, lambda e, pu=du[g][m], j=j: e.activation(
                            out=U[:, j, 16:16 + T], in_=pu.ap(), func=AF.Identity, bias=cst(C_BIN + j)),
                            reads=[du[g][m].buf, b_CST], writes=[b_U[j]])
                    else:
                        P.op(ACT, lambda e, pu=du[g][m], j=j: e.activation(
                            out=U[:, j, 0:16], in_=pu.ap(16, 32), func=AF.Identity, bias=cst(C_BIN + j)),
                            reads=[du[g][m].buf, b_CST], writes=[b_U[j]])
                        P.op(ACT, lambda e, j=j: e.activation(out=U[:, j, 0:16], in_=U[:, j, 0:16], func=AF.Copy,
                                                              scale=cst(C_MASK)),
                             reads=[b_U[j], b_CST], writes=[b_U[j]])
        P.op(ACT, lambda e: e.copy(out=UH[:], in_=U[:, :, T:T + 16]), reads=b_U, writes=[b_UH])
        WU = T + 16
        for gi in range(4):
            w = 2 << gi
            j0 = 2 * gi
            ub = [b_U[j0], b_U[j0 + 1]]
            src = U[:, j0:j0 + 2, :]
            src_b = ub
            sh = 1
            off = 0
            for step in range(gi + 1):
                p = npt()
                off2 = off + sh
                P.op(DVE, lambda e, p=p, src=src, off=off, off2=off2, sh=sh: e.tensor_tensor(
                    out=PT[p][:, :, off2:WU], in0=src[:, :, off2:WU], in1=src[:, :, off:WU - sh], op=ALU.add),
                    reads=src_b, writes=[b_PT[p]])
                src = PT[p]
                src_b = [b_PT[p]]
                off = off2
                sh *= 2
            if first:
                for m in range(2):
                    P.op(DVE, lambda e, src=src, m=m, gi=gi: e.tensor_tensor(
                        out=src[:, m, 16:16 + HALO], in0=src[:, m, 16:16 + HALO],
                        in1=CST[:, C_INVC + gi * HALO:C_INVC + (gi + 1) * HALO], op=ALU.mult),
                        reads=src_b + [b_CST], writes=src_b)
                P.op(DVE, lambda e, src=src, j0=j0: e.tensor_tensor(
                    out=PLD[:, j0:j0 + 2, 0:HALO], in0=src[:, :, 16:16 + HALO], in1=U[:, j0:j0 + 2, 16:16 + HALO],
                    op=ALU.subtract), reads=src_b + ub, writes=[b_PC[j0], b_PC[j0 + 1]])
                lo = HALO
            else:
                lo = 0
            P.op(DVE, lambda e, src=src, j0=j0, w=w, lo=lo: e.scalar_tensor_tensor(
                out=PLD[:, j0:j0 + 2, lo:T], in0=src[:, :, 16 + lo:16 + T], scalar=1.0 / w,
                in1=U[:, j0:j0 + 2, 16 + lo:16 + T], op0=ALU.mult, op1=ALU.subtract),
                reads=src_b + ub, writes=[b_PC[j0], b_PC[j0 + 1]])
        for b in range(4):
            da = proj(dr["w_in"], NKD, POOLW + b * 256, rhs_H, H_bufs, groups)
            dg = proj(dr["w_in"], NKD, POOLW + CONVW + b * 256, rhs_H, H_bufs, groups)
            for m in range(2):
                j = 2 * b + m
                for g in groups:
                    n = ncols(g)
                    t = ntmp()
                    P.op(ACT, lambda e, t=t, pg=dg[g][m], n=n, j=j: e.activation(
                        out=TMP[t][:, 0:n], in_=pg.ap(), func=AF.Sigmoid, bias=cst(C_BIN + 16 + j)),
                        reads=[dg[g][m].buf, b_CST], writes=[b_TMP[t]])
                    dst = G[:, j, HALO:WCOL] if g == "m" else G[:, j, 0:HALO]
                    P.op(DVE, lambda e, t=t, pa=da[g][m], n=n, j=j, dst=dst: e.scalar_tensor_tensor(
                        out=dst, in0=pa.ap(), scalar=cst(C_BIN + 8 + j), in1=TMP[t][:, 0:n],
                        op0=ALU.add, op1=ALU.mult),
                        reads=[da[g][m].buf, b_TMP[t], b_CST], writes=[b_G[j]])
                    if g == "h":
                        P.op(DVE, lambda e, dst=dst: e.tensor_scalar(out=dst, in0=dst, scalar1=cst(C_MASK), scalar2=None,
                                                                     op0=ALU.mult),
                             reads=[b_G[j], b_CST], writes=[b_G[j]])
        P.op(DVE, lambda e: e.tensor_copy(out=GH[:], in_=G[:, :, T:WCOL]), reads=b_G, writes=[b_GH])
        for gi in range(4):
            dmix = proj_grp(gi)
            for m in range(2):
                j = 2 * gi + m
                P.op(ACT, lambda e, pm=dmix[m], j=j: e.activation(out=MX[:, j, :], in_=pm.ap(), func=AF.Copy,
                                                                   scale=cst(C_PSC + j)),
                     reads=[dmix[m].buf, b_CST], writes=[b_MX[j]])
        MXb = {"m": b_MX}
        CNb = {"m": b_PC}
        Hm = {"m": b_H["m"]}
        def conv_op(n):
            i, j = divmod(n, 8)
            if i == 0:
                P.op(DVE, lambda e, j=j: e.tensor_scalar(
                    out=C[:, j, :], in0=G[:, j, 2:2 + T], scalar1=cst(C_CVK + j * KCONV), scalar2=cst(C_CVB + j),
                    op0=ALU.mult, op1=ALU.add), reads=[b_G[j], b_CST], writes=[b_C[j]])
            else:
                P.op(DVE, lambda e, j=j, i=i: e.scalar_tensor_tensor(
                    out=C[:, j, :], in0=G[:, j, 2 + i:2 + i + T], scalar=cst(C_CVK + j * KCONV + i), in1=C[:, j, :],
                    op0=ALU.mult, op1=ALU.add), reads=[b_G[j], b_C[j], b_CST], writes=[b_C[j]])

        for cb in range(8):
            dga = proj(dr["w_in"], NKD, POOLW + 2 * CONVW + cb * 256, rhs_H, Hm, ["m"])
            da = proj(dr["pool_w_proj"], 8, cb * 256, lambda k, g: MX[:, k, :], MXb, ["m"])
            ts = []
            for m in range(2):
                j = 2 * cb + m
                t = ntmp()
                ts.append(t)
                P.op(ACT, lambda e, t=t, pg=dga["m"][m], j=j: e.activation(
                    out=TMP[t][:, 0:T], in_=pg.ap(), func=AF.Sigmoid, bias=cst(C_BIN + 24 + j)),
                    reads=[dga["m"][m].buf, b_CST], writes=[b_TMP[t]])
            dgb = proj(dr["w_in"], NKD, POOLW + 2 * CONVW + D + cb * 256, rhs_H, Hm, ["m"])
            for m in range(2):
                j = 2 * cb + m
                P.op(ACT, lambda e, pg=dgb["m"][m], j=j: e.activation(
                    out=SGB[:, j, :], in_=pg.ap(), func=AF.Sigmoid, bias=cst(C_BIN + 40 + j)),
                    reads=[dgb["m"][m].buf, b_CST], writes=[b_SGB[j]])
            for n in range(cb * KCONV, (cb + 1) * KCONV):
                conv_op(n)
            for m in range(2):
                j = 2 * cb + m
                t = ts[m]
                P.op(DVE, lambda e, t=t, p_=da["m"][m], j=j: e.tensor_tensor(
                    out=MERGED[:, j, :], in0=p_.ap(), in1=TMP[t][:, 0:T], op=ALU.mult),
                    reads=[da["m"][m].buf, b_TMP[t]], writes=[b_M[j]] + b_U)
        d = PS("m")
        for j in range(8):
            P.op(PE, lambda e, j=j: e.matmul(d.ap(), ONES[:], C[:, j, :], start=(j == 0), stop=(j == 7)),
                 reads=[b_C[j], b_ONES], writes=[d.buf])
        P.op(ACT, lambda e: e.activation(out=MEAN[:], in_=d.ap(), func=AF.Copy, scale=1.0 / CONVW),
             reads=[d.buf], writes=[b_MEAN])
        for j in range(8):
            P.op(DVE, lambda e, j=j: e.tensor_tensor(out=C[:, j, :], in0=C[:, j, :], in1=MEAN[:], op=ALU.subtract),
                 reads=[b_C[j], b_MEAN], writes=[b_C[j]])
        d2 = stats(lambda j, g: C[:, j, :], lambda j, g: b_C[j], 8, "m", direct=True)
        rstd_from(d2, CONVW, "m", RS[:, 0:T], b_RS["m"])
        for j in range(8):
            P.op(DVE, lambda e, j=j: e.tensor_tensor(out=C[:, j, :], in0=C[:, j, :], in1=RS[:, 0:T], op=ALU.mult),
                 reads=[b_C[j], b_RS["m"]], writes=[b_C[j]])
            P.op(ACT, lambda e, j=j: e.activation(out=CN[:, j, :], in_=C[:, j, :], func=AF.Silu,
                                                  bias=cst(C_LNB + j), scale=cst(C_LNG + j)),
                 reads=[b_C[j], b_CST], writes=[b_PC[j]])
        for cb in range(8):
            dbb = proj(dr["conv_w_proj"], 8, cb * 256, lambda k, g: CN[:, k, :], CNb, ["m"])
            for m in range(2):
                j = 2 * cb + m
                t = ntmp()
                P.op(DVE, lambda e, t=t, p_=dbb["m"][m], j=j: e.tensor_tensor(
                    out=TMP[t][:, 0:T], in0=p_.ap(), in1=SGB[:, j, :], op=ALU.mult),
                    reads=[dbb["m"][m].buf, b_SGB[j]], writes=[b_TMP[t]])
                P.op(DVE, lambda e, t=t, j=j: e.tensor_tensor(
                    out=MERGED[:, j, :], in0=TMP[t][:, 0:T], in1=MERGED[:, j, :], op=ALU.add),
                    reads=[b_TMP[t], b_M[j]], writes=[b_M[j]])
        xs = XStats(["m"], hp_cg=C_N3)
        for cb in range(8):
            dy = proj(dr["w_out"], NKD, cb * 256, lambda k, g: MERGED[:, k, :], {"m": b_M + b_U}, ["m"])
            for m in range(2):
                j = 2 * cb + m
                P.op(DVE, lambda e, py=dy["m"][m], j=j: e.tensor_tensor(
                    out=X[:, j, 0:T], in0=py.ap(), in1=X[:, j, 0:T], op=ALU.add),
                    reads=[dy["m"][m].buf, b_X["m"][j]], writes=[b_X["m"][j]])
            xs.feed([2 * cb, 2 * cb + 1])
        return xs

    def proj_grp(gi):
        j0 = 2 * gi
        wt = dr["pool_w_grp"]
        d = proj(wt[gi * 256:(gi + 1) * 256, :], 2, 0,
                 lambda k, g: PLD[:, j0 + k, :], {"m": [b_PC[j0], b_PC[j0 + 1]]}, ["m"])
        return d["m"]

    xT = dr["xT"]
    outT = dr["outT"]
    def load_x_group(ti, q):
        c0 = HALO + ti * T
        src = xT[q * 512:(q + 1) * 512, c0:c0 + T].rearrange("(kc p) t -> p kc t", p=128)
        P.op(SP, lambda e, src=src, q=q: e.dma_start(out=X[:, 4 * q:4 * q + 4, 0:T], in_=src),
             writes=b_X["m"][4 * q:4 * q + 4], semkey=s_xq[q], incval=16)

    def finalize(ti, d3):
        rstd_from(d3["m"], D, "m", RS2[:], b_RS2)
        for j in range(NKD):
            P.op(DVE, lambda e, j=j: e.scalar_tensor_tensor(
                out=OUTB[:, j, :], in0=XF(j), scalar=cst(C_NF + j), in1=RS2[:],
                op0=ALU.mult, op1=ALU.mult), reads=xf_bufs(j) + [b_RS2, b_CST],
                writes=(bA_hi + [b_OB[0]]) if j == 0 else [b_OB[j // 4]])
            if j % 4 == 3:
                q = j // 4
                dsto = outT[q * 512:(q + 1) * 512, ti * T:(ti + 1) * T].rearrange("(kc p) t -> p kc t", p=128)
                P.op(SP, lambda e, dsto=dsto, q=q: e.dma_start(out=dsto, in_=OUTB[:, 4 * q:4 * q + 4, :]),
                     reads=[b_OB[q]], writes=[b_OUTq[q]], semkey=s_oq[q], incval=16)

    for q in range(4):
        load_x_group(0, q)
    srch = xT[:, 0:HALO].rearrange("(kc p) t -> p kc t", p=128)
    P.op(SP, lambda e: e.dma_start(out=X[:, :, T:WCOL], in_=srch), writes=b_X["h"], semkey=s_xh, incval=16)
    xs0 = XStats(["m", "h"], hp_cg=C_N1)
    for j in range(NKD):
        xs0.feed([j])
    d0 = xs0
    pending = None
    for ti in range(n_tiles):
        groups = ["m", "h"] if ti == 0 else ["m"]
        d1 = ffn(C_N1, dr["ffn1_w_in"], dr["ffn1_w_out"], groups, d0, groups, after_norm=pending)
        d2 = mixer(ti, groups, d1)
        if ti + 1 < n_tiles:
            d3, d0 = ffn(C_N3, dr["ffn2_w_in"], dr["ffn2_w_out"], ["m"], d2, ["m"], final=True,
                         next_load=lambda q, ti=ti: load_x_group(ti + 1, q))
            pending = (lambda ti=ti, d3=d3: finalize(ti, d3))
        else:
            d3 = ffn(C_N3, dr["ffn2_w_in"], dr["ffn2_w_out"], ["m"], d2, ["m"], final=True)
            finalize(ti, d3)
    P.wait_for(SP, b_OUTq)
    P.emit_all()
    return P


def build_nc(n_tiles=4):
    nc = bass.Bass("TRN2", target_bir_lowering=False)
    ntok = n_tiles * T
    dr = {}

    def din(name, shape):
        dr[name] = nc.dram_tensor(name, list(shape), F32, kind="ExternalInput").ap()

    din("xT", (D, HALO + ntok))
    din("cst", (128, NCST))
    din("ffn1_w_in", (D, 2 * DFF))
    din("ffn1_w_out", (DFF, D))
    din("w_in", (D, INW))
    din("pool_w_grp", (POOLW, 256))
    din("pool_w_proj", (POOLW, D))
    din("conv_w_proj", (CONVW, D))
    din("w_out", (D, D))
    din("ffn2_w_in", (D, 2 * DFF))
    din("ffn2_w_out", (DFF, D))
    dr["outT"] = nc.dram_tensor("outT", [D, ntok], F32, kind="ExternalOutput").ap()
    with ExitStack() as st:
        emit_program(nc, st, dr, n_tiles)
    return nc


def pack_consts(inp, seq_start):
    c = np.zeros((128, NCST), np.float32)

    def put(col, vec):
        n = vec.shape[0] // 128
        c[:, col:col + n] = vec.reshape(n, 128).T

    put(C_N1, inp["ffn1_norm"].reshape(-1))
    put(C_N2, inp["mix_norm"].reshape(-1))
    put(C_N3, inp["ffn2_norm"].reshape(-1))
    put(C_NF, inp["final_norm"].reshape(-1))
    put(C_BIN, inp["b_in"].reshape(-1))
    put(C_PSC, inp["pool_scale"].reshape(-1))
    put(C_CVB, inp["conv_b"].reshape(-1))
    put(C_LNG, inp["conv_ln_g"].reshape(-1))
    put(C_LNB, inp["conv_ln_b"].reshape(-1))
    k = inp["conv_dw"].reshape(KCONV, CONVW)
    c[:, C_CVK:C_CVK + 8 * KCONV] = k.reshape(KCONV, 8, 128).transpose(2, 1, 0).reshape(128, 8 * KCONV)
    c[:, C_MASK] = 0.0 if seq_start else 1.0
    c[:, C_EPS] = EPS
    for gi in range(4):
        w = 2 << gi
        for t in range(HALO):
            cnt = min(t + 1, w) if seq_start else w
            c[:, C_INVC + gi * HALO + t] = 1.0 / cnt
    return c


def make_in_maps(inp, n_cores=8, n_tiles=4):
    x = np.asarray(inp["x"], np.float32)
    shared = {
        "ffn1_w_in": np.ascontiguousarray(np.asarray(inp["ffn1_w_in"], np.float32).reshape(D, 2 * DFF)),
        "ffn1_w_out": np.ascontiguousarray(np.asarray(inp["ffn1_w_out"], np.float32).reshape(DFF, D)),
        "w_in": np.ascontiguousarray(np.asarray(inp["w_in"], np.float32).reshape(D, INW)),
        "pool_w_grp": np.ascontiguousarray(np.asarray(inp["pool_w_grp"], np.float32).reshape(POOLW, 256)),
        "pool_w_proj": np.ascontiguousarray(np.asarray(inp["pool_w_proj"], np.float32).reshape(POOLW, D)),
        "conv_w_proj": np.ascontiguousarray(np.asarray(inp["conv_w_proj"], np.float32).reshape(CONVW, D)),
        "w_out": np.ascontiguousarray(np.asarray(inp["w_out"], np.float32).reshape(D, D)),
        "ffn2_w_in": np.ascontiguousarray(np.asarray(inp["ffn2_w_in"], np.float32).reshape(D, 2 * DFF)),
        "ffn2_w_out": np.ascontiguousarray(np.asarray(inp["ffn2_w_out"], np.float32).reshape(DFF, D)),
    }
    small = {k: np.asarray(v, np.float32) for k, v in inp.items()
             if k in ("ffn1_norm", "mix_norm", "ffn2_norm", "final_norm", "b_in", "pool_scale", "conv_b",
                      "conv_ln_g", "conv_ln_b", "conv_dw")}
    cst = {True: pack_consts(small, True), False: pack_consts(small, False)}
    ntok = n_tiles * T
    maps = []
    for c in range(n_cores):
        b, hf = divmod(c, 2)
        t0 = hf * TOK_CORE
        xt = np.zeros((D, HALO + ntok), np.float32)
        xt[:, HALO:] = x[b, t0:t0 + ntok, :].T
        if t0 > 0:
            xt[:, :HALO] = x[b, t0 - HALO:t0, :].T
        m = {"xT": xt, "cst": cst[t0 == 0]}
        m.update(shared)
        maps.append(m)
    return maps


_NC_CACHE = {}


def kernel(**inputs):
    if "nc" not in _NC_CACHE:
        _NC_CACHE["nc"] = build_nc(4)
    nc = _NC_CACHE["nc"]
    maps = make_in_maps(inputs, 8, 4)
    res = run_bass_kernel_spmd(nc, maps, core_ids=list(range(8)))
    out = np.empty((BATCH, SEQ, D), np.float32)
    for c in range(8):
        b, hf = divmod(c, 2)
        out[b, hf * TOK_CORE:(hf + 1) * TOK_CORE, :] = res.results[c]["outT"].T
    return out
```
